# Optimizing a Trainium2 kernel written in Bass

```python
import jax
import jax.numpy as jnp
from jax import lax
import numpy as np

D_MODEL = 1024
BATCH = 4
SEQ = 8192
DEPTH = 2

N_MIXERS = 4
GROUP_WIDTH = D_MODEL // N_MIXERS
HEAD_DIM = 64
N_HEADS = GROUP_WIDTH // HEAD_DIM
D_FF = ((8 * D_MODEL // 3 + 127) // 128) * 128
CONV_WIDTH = 3
CHUNK = 128
ROPE_BASE = 10000.0
RET_DECAY_EXP0 = 5
RWKV_W_RANK = 32
RWKV_A_RANK = 32
RWKV_G_RANK = 64
RWKV_COLS = 3 * GROUP_WIDTH + 2 * RWKV_W_RANK + 2 * RWKV_A_RANK + RWKV_G_RANK
MLSTM_COLS = 4 * GROUP_WIDTH + 4 * N_HEADS
P_TOTAL = 3 * GROUP_WIDTH + 4 * GROUP_WIDTH + RWKV_COLS + MLSTM_COLS
RMS_EPS = 1e-6
RWKV_GN_EPS = 64e-5

kernel_name = 'hybrid_parallel_mixer_encoder'


def split_cols(t, widths):
    points, acc = [], 0
    for w in widths[:-1]:
        acc += w
        points.append(acc)
    return jnp.split(t, points, axis=-1)


def rms_norm(x, g):
    xf = x.astype(jnp.float32)
    y = xf * lax.rsqrt(jnp.mean(xf * xf, axis=-1, keepdims=True) + RMS_EPS)
    return (y * g.astype(jnp.float32)).astype(x.dtype)


def swiglu(h, w_gate, w_up, w_down):
    return (jax.nn.silu(h @ w_gate) * (h @ w_up)) @ w_down


def heads(t):
    b, s, _ = t.shape
    return t.reshape(b, s, N_HEADS, HEAD_DIM).transpose(0, 2, 1, 3)


def merge_heads(t):
    b, h, s, d = t.shape
    return t.transpose(0, 2, 1, 3).reshape(b, s, h * d)


def to_chunks(t):
    b, h, s = t.shape[:3]
    t = t.reshape(b, h, s // CHUNK, CHUNK, *t.shape[3:])
    return jnp.moveaxis(t, 2, 0)


def from_chunks(t):
    t = jnp.moveaxis(t, 0, 2)
    b, h, nc, l = t.shape[:4]
    return t.reshape(b, h, nc * l, *t.shape[4:])


def rope_tables(seq):
    inv = ROPE_BASE ** (-jnp.arange(0, HEAD_DIM, 2, dtype=jnp.float32) / HEAD_DIM)
    ang = jnp.arange(seq, dtype=jnp.float32)[:, None] * inv[None, :]
    return jnp.cos(ang), jnp.sin(ang)


def rotary(t, cos, sin):
    t1, t2 = jnp.split(t, 2, axis=-1)
    return jnp.concatenate([t1 * cos - t2 * sin, t1 * sin + t2 * cos], axis=-1)


def short_conv_mixer(zb, zc, zh, conv_w, conv_b):
    u = zc * zh
    y = lax.conv_general_dilated(
        u, conv_w[:, None, :].astype(u.dtype), window_strides=(1,),
        padding=[(CONV_WIDTH // 2, CONV_WIDTH // 2)],
        dimension_numbers=('NWC', 'WIO', 'NWC'), feature_group_count=GROUP_WIDTH)
    return zb * (y + conv_b.astype(y.dtype))


def retention_scan(q, k, v, log_gamma, include_diag):
    idx = jnp.arange(CHUNK, dtype=jnp.float32)
    rel = idx[:, None] - idx[None, :]
    mask = (rel >= 0) if include_diag else (rel > 0)
    intra = jnp.where(mask, jnp.exp(jnp.where(mask, rel, 0.0) * log_gamma[:, None, None]), 0.0)
    q_dec = jnp.exp((idx + 1.0) * log_gamma[:, None])
    k_dec = jnp.exp((CHUNK - 1.0 - idx) * log_gamma[:, None])
    c_dec = jnp.exp(CHUNK * log_gamma)

    def step(state, inp):
        qc, kc, vc = inp
        s = jnp.einsum('bhtd,bhsd->bhts', qc, kc) * intra
        out = (jnp.einsum('bhts,bhsv->bhtv', s, vc)
               + q_dec[:, :, None] * jnp.einsum('bhtk,bhkv->bhtv', qc, state))
        state = c_dec[:, None, None] * state + jnp.einsum('bhsk,bhsv->bhkv', kc * k_dec[:, :, None], vc)
        return state, out

    b, h, _, d = q.shape
    state0 = jnp.zeros((b, h, d, d), jnp.float32)
    _, out = lax.scan(step, state0, (to_chunks(q), to_chunks(k), to_chunks(v)))
    return from_chunks(out)


def retention_mixer(zq, zk, zv, zg, decay_logit, cos, sin):
    f32 = jnp.float32
    q = rotary(heads(zq.astype(f32)), cos, sin) * HEAD_DIM ** -0.5
    k = rotary(heads(zk.astype(f32)), cos, sin)
    v = heads(zv.astype(f32))
    log_gamma = jax.nn.log_sigmoid(decay_logit.astype(f32))
    fwd = retention_scan(q, k, v, log_gamma[0], True)
    bwd = jnp.flip(retention_scan(jnp.flip(q, 2), jnp.flip(k, 2), jnp.flip(v, 2), log_gamma[1], False), 2)
    o = fwd + bwd
    o = o * lax.rsqrt(jnp.mean(o * o, axis=-1, keepdims=True) + RMS_EPS)
    return (jax.nn.silu(zg.astype(f32)) * merge_heads(o)).astype(zq.dtype)


def rwkv7_scan(r, w, k, v, kk, a, reverse):
    def step(state, inp):
        rt, wt, kt, vt, kkt, at = inp
        sa = jnp.einsum('bhvk,bhk->bhv', state, -kkt)
        state = (state * wt[:, :, None, :] + sa[..., None] * (kkt * at)[:, :, None, :]
                 + vt[..., None] * kt[:, :, None, :])
        return state, jnp.einsum('bhvk,bhk->bhv', state, rt)

    b, _, h, n = r.shape
    state0 = jnp.zeros((b, h, n, n), jnp.float32)
    xs = (jnp.moveaxis(r, 1, 0), jnp.moveaxis(w, 1, 0), jnp.moveaxis(k, 1, 0),
          jnp.moveaxis(v, 1, 0), jnp.moveaxis(kk, 1, 0), jnp.moveaxis(a, 1, 0))
    _, out = lax.scan(step, state0, xs, reverse=reverse)
    return jnp.moveaxis(out, 0, 1)


def rwkv7_mixer(zw, mu, w0, w2, a0, a2, g2, k_k, k_a, r_k, ln_w, ln_b):
    f32 = jnp.float32
    G = GROUP_WIDTH
    b, s, _ = zw.shape
    z = zw.astype(f32)
    zp = jnp.pad(z, ((0, 0), (1, 1), (0, 0)))
    z = z + mu * (0.5 * (zp[:, :-2] + zp[:, 2:]) - z)
    r, k, v, wl, al, gl = split_cols(z, [G, G, G, 2 * RWKV_W_RANK, 2 * RWKV_A_RANK, RWKV_G_RANK])
    wl = wl.reshape(b, s, 2, RWKV_W_RANK)
    al = al.reshape(b, s, 2, RWKV_A_RANK)
    w_log = -jax.nn.softplus(-(w0 + jnp.einsum('bsdr,drc->bsdc', jnp.tanh(wl), w2))) - 0.5
    decay = jnp.exp(-jnp.exp(w_log))
    a = jax.nn.sigmoid(a0 + jnp.einsum('bsdr,drc->bsdc', al, a2))
    gate = jax.nn.sigmoid(gl) @ g2
    hs = lambda t: t.reshape(b, s, N_HEADS, HEAD_DIM)
    kk = hs(k * k_k)
    kk = kk / jnp.maximum(jnp.sqrt(jnp.sum(kk * kk, axis=-1, keepdims=True)), 1e-12)
    rh, vh = hs(r), hs(v)
    wkv = jnp.zeros_like(rh)
    bonus = jnp.zeros_like(rh)
    for d, rev in ((0, False), (1, True)):
        kd = hs(k * (1.0 + (a[:, :, d] - 1.0) * k_a))
        wkv = wkv + rwkv7_scan(rh, hs(decay[:, :, d]), kd, vh, kk, hs(a[:, :, d]), rev)
        bonus = bonus + jnp.sum(rh * kd * r_k, axis=-1, keepdims=True) * vh
    mean = jnp.mean(wkv, axis=-1, keepdims=True)
    var = jnp.mean((wkv - mean) ** 2, axis=-1, keepdims=True)
    wkv = (wkv - mean) * lax.rsqrt(var + RWKV_GN_EPS)
    out = wkv.reshape(b, s, G) * ln_w + ln_b + bonus.reshape(b, s, G)
    return (out * gate).astype(zw.dtype)


def mlstm_chunkwise(q, k, v, log_i, log_f):
    b, h, _, d = q.shape
    tril = jnp.tril(jnp.ones((CHUNK, CHUNK), dtype=bool))

    def step(carry, inp):
        c_mat, n_vec, m_prev = carry
        qc, kc, vc, li, lf = inp
        bcum = jnp.cumsum(lf, axis=-1)
        d_log = jnp.where(tril, bcum[..., :, None] - bcum[..., None, :] + li[..., None, :], -jnp.inf)
        inter_log = bcum + m_prev[..., None]
        m_t = jnp.maximum(inter_log, jnp.max(d_log, axis=-1))
        inter_w = jnp.exp(inter_log - m_t)
        s = jnp.einsum('bhtd,bhsd->bhts', qc, kc) * jnp.exp(d_log - m_t[..., None])
        num = (jnp.einsum('bhts,bhsv->bhtv', s, vc)
               + inter_w[..., None] * jnp.einsum('bhvk,bhtk->bhtv', c_mat, qc))
        den = jnp.sum(s, axis=-1) + inter_w * jnp.einsum('bhk,bhtk->bht', n_vec, qc)
        out = num / jnp.maximum(jnp.abs(den), jnp.exp(-m_t))[..., None]
        b_end = bcum[..., -1]
        w_log = b_end[..., None] - bcum + li
        m_new = jnp.maximum(b_end + m_prev, jnp.max(w_log, axis=-1))
        old_w = jnp.exp(b_end + m_prev - m_new)
        new_w = jnp.exp(w_log - m_new[..., None])
        c_mat = old_w[..., None, None] * c_mat + jnp.einsum('bhs,bhsv,bhsk->bhvk', new_w, vc, kc)
        n_vec = old_w[..., None] * n_vec + jnp.einsum('bhs,bhsk->bhk', new_w, kc)
        return (c_mat, n_vec, m_new), out

    carry0 = (jnp.zeros((b, h, d, d), jnp.float32), jnp.zeros((b, h, d), jnp.float32),
              jnp.zeros((b, h), jnp.float32))
    _, out = lax.scan(step, carry0, (to_chunks(q), to_chunks(k), to_chunks(v),
                                     to_chunks(log_i), to_chunks(log_f)))
    return from_chunks(out)


def mlstm_mixer(zq, zk, zv, zo, zi, zf, i_bias, f_bias, norm_w):
    f32 = jnp.float32
    b, s, _ = zq.shape
    q = heads(zq.astype(f32)) * HEAD_DIM ** -0.5
    k = heads(zk.astype(f32))
    v = heads(zv.astype(f32))
    gi = zi.astype(f32).reshape(b, s, 2, N_HEADS) + i_bias
    gf = zf.astype(f32).reshape(b, s, 2, N_HEADS) + f_bias
    log_i = jnp.transpose(gi, (2, 0, 3, 1))
    log_f = jnp.transpose(jax.nn.log_sigmoid(gf), (2, 0, 3, 1))
    fwd = mlstm_chunkwise(q, k, v, log_i[0], log_f[0])
    bwd = jnp.flip(mlstm_chunkwise(jnp.flip(q, 2), jnp.flip(k, 2), jnp.flip(v, 2),
                                   jnp.flip(log_i[1], -1), jnp.flip(log_f[1], -1)), 2)
    hsum = fwd + bwd
    hsum = hsum * lax.rsqrt(jnp.mean(hsum * hsum, axis=-1, keepdims=True) + RMS_EPS)
    return (jax.nn.sigmoid(zo.astype(f32)) * merge_heads(hsum) * norm_w).astype(zq.dtype)


def setup_inputs(seed: int = 0) -> dict:
    key = jax.random.key(seed)
    ks = jax.random.split(key, 32)
    f32 = jnp.float32
    G = GROUP_WIDTH
    nrm = lambda k, shape, fan_in: jax.random.normal(k, shape, f32) * fan_in ** -0.5
    noise = lambda k, shape, sc: sc * jax.random.normal(k, shape, f32)
    p = jnp.arange(RET_DECAY_EXP0, RET_DECAY_EXP0 + N_HEADS, dtype=f32)
    return {
        'x': jax.random.normal(ks[0], (BATCH, SEQ, D_MODEL), f32),
        'norm_g': 1.0 + noise(ks[1], (DEPTH, 6, D_MODEL), 0.02),
        'ffn_w_gate': nrm(ks[2], (DEPTH, 2, D_MODEL, D_FF), D_MODEL),
        'ffn_w_up': nrm(ks[3], (DEPTH, 2, D_MODEL, D_FF), D_MODEL),
        'ffn_w_down': nrm(ks[4], (DEPTH, 2, D_FF, D_MODEL), D_FF),
        'w_in': nrm(ks[5], (DEPTH, D_MODEL, P_TOTAL), D_MODEL),
        'w_out': nrm(ks[6], (DEPTH, D_MODEL, D_MODEL), D_MODEL),
        'conv_w': nrm(ks[7], (DEPTH, CONV_WIDTH, G), CONV_WIDTH),
        'conv_b': noise(ks[8], (DEPTH, G), 0.02),
        'ret_decay_logit': jnp.log(2.0 ** p - 1.0) + noise(ks[9], (DEPTH, 2, N_HEADS), 0.05),
        'rwkv_mu': jax.random.uniform(ks[10], (DEPTH, RWKV_COLS), f32),
        'rwkv_w0': jax.random.uniform(ks[11], (DEPTH, 2, G), f32, minval=-6.0, maxval=-1.0),
        'rwkv_w2': nrm(ks[12], (DEPTH, 2, RWKV_W_RANK, G), RWKV_W_RANK),
        'rwkv_a0': noise(ks[13], (DEPTH, 2, G), 0.1),
        'rwkv_a2': nrm(ks[14], (DEPTH, 2, RWKV_A_RANK, G), RWKV_A_RANK),
        'rwkv_g2': nrm(ks[15], (DEPTH, RWKV_G_RANK, G), RWKV_G_RANK),
        'rwkv_k_k': 0.85 + noise(ks[16], (DEPTH, G), 0.02),
        'rwkv_k_a': 1.0 + noise(ks[17], (DEPTH, G), 0.02),
        'rwkv_r_k': noise(ks[18], (DEPTH, N_HEADS, HEAD_DIM), 0.1),
        'rwkv_ln_w': 1.0 + noise(ks[19], (DEPTH, G), 0.02),
        'rwkv_ln_b': noise(ks[20], (DEPTH, G), 0.02),
        'mlstm_i_bias': noise(ks[21], (DEPTH, 2, N_HEADS), 0.1),
        'mlstm_f_bias': jax.random.uniform(ks[22], (DEPTH, 2, N_HEADS), f32, minval=3.0, maxval=6.0),
        'mlstm_norm_w': 1.0 + noise(ks[23], (DEPTH, G), 0.02),
    }


def reference(x, norm_g, ffn_w_gate, ffn_w_up, ffn_w_down, w_in, w_out, conv_w, conv_b,
              ret_decay_logit, rwkv_mu, rwkv_w0, rwkv_w2, rwkv_a0, rwkv_a2, rwkv_g2,
              rwkv_k_k, rwkv_k_a, rwkv_r_k, rwkv_ln_w, rwkv_ln_b,
              mlstm_i_bias, mlstm_f_bias, mlstm_norm_w):
    G = GROUP_WIDTH
    cos, sin = rope_tables(x.shape[1])
    widths = [G, G, G,
              G, G, G, G,
              RWKV_COLS,
              G, G, G, G, 2 * N_HEADS, 2 * N_HEADS]
    for l in range(DEPTH):
        g = norm_g[l]
        h = rms_norm(x, g[0])
        x = x + 0.5 * rms_norm(swiglu(h, ffn_w_gate[l, 0], ffn_w_up[l, 0], ffn_w_down[l, 0]), g[1])
        h = rms_norm(x, g[2])
        z = h @ w_in[l]
        cb, cc, ch, rq, rk, rv, rg, zw, mq, mk, mv, mo, mi, mf = split_cols(z, widths)
        y_conv = short_conv_mixer(cb, cc, ch, conv_w[l], conv_b[l])
        y_ret = retention_mixer(rq, rk, rv, rg, ret_decay_logit[l], cos, sin)
        y_rwkv = rwkv7_mixer(zw, rwkv_mu[l], rwkv_w0[l], rwkv_w2[l], rwkv_a0[l], rwkv_a2[l],
                             rwkv_g2[l], rwkv_k_k[l], rwkv_k_a[l], rwkv_r_k[l],
                             rwkv_ln_w[l], rwkv_ln_b[l])
        y_mlstm = mlstm_mixer(mq, mk, mv, mo, mi, mf, mlstm_i_bias[l], mlstm_f_bias[l], mlstm_norm_w[l])
        y = jnp.concatenate([y_conv.astype(z.dtype), y_ret.astype(z.dtype),
                             y_rwkv.astype(z.dtype), y_mlstm.astype(z.dtype)], axis=-1) @ w_out[l]
        x = x + rms_norm(y, g[3])
        h = rms_norm(x, g[4])
        x = x + 0.5 * rms_norm(swiglu(h, ffn_w_gate[l, 1], ffn_w_up[l, 1], ffn_w_down[l, 1]), g[5])
    return x
```

```python
import numpy as np
from contextlib import ExitStack
import concourse.bass as bass
import concourse.mybir as mybir
from concourse.bass_utils import run_bass_kernel_spmd

F32 = mybir.dt.float32
BF16 = mybir.dt.bfloat16
AF = mybir.ActivationFunctionType
ALU = mybir.AluOpType
AX = mybir.AxisListType

ENGS = ("pe", "act", "dve", "pool", "sp")
N_DMA_SEMS = 24


class View:
    __slots__ = ("buf", "ap")

    def __init__(self, buf, ap):
        self.buf = buf
        self.ap = ap

    def __getitem__(self, k):
        return View(self.buf, self.ap[k])

    def rearrange(self, s, **kw):
        return View(self.buf, self.ap.rearrange(s, **kw))

    def bitcast(self, dt):
        return View(self.buf, self.ap.bitcast(dt))

    def to_broadcast(self, shape):
        return View(self.buf, self.ap.to_broadcast(shape))

    def broadcast_to(self, shape):
        return View(self.buf, self.ap.broadcast_to(shape))

    def partition_broadcast(self, n):
        return View(self.buf, self.ap.partition_broadcast(n))

    def unsqueeze(self, a):
        return View(self.buf, self.ap.unsqueeze(a))

    @property
    def shape(self):
        return self.ap.shape


class Buf:
    def __init__(self, ap, name):
        self.full = ap
        self.name = name
        self.w = None
        self.r = {}

    def __getitem__(self, k):
        return View(self, self.full[k])

    def v(self):
        return View(self, self.full)


class Prog:
    def __init__(self, nc, same_engine_sync=True):
        self.nc = nc
        self.es = ExitStack()
        self.same = same_engine_sync
        self.ops = {e: [] for e in ENGS}
        self.cnt = {e: 0 for e in ENGS}
        self.waited = {e: {} for e in ENGS}
        self.sems = {}
        for e in ENGS:
            self.sems[e] = self.es.enter_context(nc.semaphore("s_" + e))
        self.dsem_val = []
        for i in range(N_DMA_SEMS):
            self.sems[("d", i)] = self.es.enter_context(nc.semaphore("s_d%d" % i))
            self.dsem_val.append(0)
        self.dnext = 0
        self.out_events = []
        self.nbuf = 0

    def sbuf(self, name, shape, dt):
        h = self.es.enter_context(self.nc.sbuf_tensor(name, list(shape), dt))
        nbytes = int(np.prod(shape[1:])) * (4 if dt == F32 else 2)
        pad = (-nbytes) % 64
        if pad:
            self.es.enter_context(self.nc.sbuf_tensor(name + "_pad", [shape[0], pad // 2], BF16))
        return Buf(h[:], name)

    def psum(self, name, shape, dt):
        h = self.es.enter_context(self.nc.psum_tensor(name, list(shape), dt))
        return Buf(h[:], name)

    def dram(self, name, shape, dt, kind="Internal"):
        h = self.nc.dram_tensor(name, list(shape), dt, kind=kind)
        return Buf(h.ap(), name)

    def _deps(self, reads, writes):
        deps = {}

        def add(ev):
            if ev is None:
                return
            k, v = ev
            if deps.get(k, 0) < v:
                deps[k] = v

        for b in reads:
            add(b.w)
        for b in writes:
            add(b.w)
            for ev in b.r.items():
                add(ev)
        return deps

    def _emit_waits(self, eng, deps, skip_self):
        wl = []
        for k, v in deps.items():
            if k == eng and skip_self:
                continue
            if self.waited[eng].get(k, 0) >= v:
                continue
            self.waited[eng][k] = v
            wl.append((k, v))
        return wl

    def _bufs(self, views):
        seen = []
        for v in views:
            if isinstance(v, View) and v.buf not in seen:
                seen.append(v.buf)
        return seen

    def op(self, eng, fn, reads, writes):
        rb = self._bufs(reads)
        wb = self._bufs(writes)
        deps = self._deps(rb, wb)
        skip_self = (eng == "pe") or (not self.same)
        waits = self._emit_waits(eng, deps, skip_self)
        self.cnt[eng] += 1
        ev = (eng, self.cnt[eng])
        self.ops[eng].append((waits, fn, eng, 1))
        for b in rb:
            if b not in wb:
                if b.r.get(ev[0], 0) < ev[1]:
                    b.r[ev[0]] = ev[1]
        for b in wb:
            b.w = ev
            b.r = {}
        return ev

    def dma(self, eng, out, in_, **kw):
        rb = self._bufs([in_])
        wb = self._bufs([out])
        deps = self._deps(rb, wb)
        i = self.dnext
        self.dnext = (self.dnext + 1) % N_DMA_SEMS
        key = ("d", i)
        if self.dsem_val[i] > 0:
            if deps.get(key, 0) < self.dsem_val[i]:
                deps[key] = self.dsem_val[i]
        waits = self._emit_waits(eng, deps, False)
        self.dsem_val[i] += 16
        ev = (key, self.dsem_val[i])
        oa, ia = out.ap, in_.ap
        self.ops[eng].append((waits, lambda e: e.dma_start(out=oa, in_=ia, **kw), key, 16))
        for b in rb:
            b.r[ev[0]] = ev[1]
        for b in wb:
            b.w = ev
            b.r = {}
        return ev

    @staticmethod
    def _a(x):
        return x.ap if isinstance(x, View) else x

    def mm(self, out, lhsT, rhs, start=True, stop=True, **kw):
        o, l, r = out.ap, lhsT.ap, rhs.ap
        rd = [lhsT, rhs] + ([] if start else [out])
        return self.op("pe", lambda e: e.matmul(o, l, r, start=start, stop=stop, **kw), rd, [out])

    def transpose(self, out, in_, ident):
        o, i, d = out.ap, in_.ap, ident.ap
        return self.op("pe", lambda e: e.transpose(o, i, d), [in_, ident], [out])

    def act(self, out, in_, func, bias=None, scale=None, accum_out=None, eng="act"):
        kw = {}
        rd = [in_]
        wr = [out]
        if bias is not None:
            kw["bias"] = self._a(bias)
            rd.append(bias)
        if scale is not None:
            kw["scale"] = self._a(scale)
            rd.append(scale)
        if accum_out is not None:
            kw["accum_out"] = accum_out.ap
            wr.append(accum_out)
        o, i = out.ap, in_.ap
        return self.op(eng, lambda e: e.activation(o, i, func, **kw), rd, wr)

    def tt(self, eng, out, in0, in1, op):
        o, a, b = out.ap, in0.ap, in1.ap
        return self.op(eng, lambda e: e.tensor_tensor(o, a, b, op), [in0, in1], [out])

    def ts(self, eng, out, in0, s1, op0, s2=None, op1=None, accum_out=None):
        o, a = out.ap, in0.ap
        a1, a2 = self._a(s1), self._a(s2)
        rd = [in0, s1, s2]
        wr = [out]
        kw = {}
        if op1 is not None:
            kw["op1"] = op1
        if accum_out is not None:
            kw["accum_out"] = accum_out.ap
            wr.append(accum_out)
        return self.op(eng, lambda e: e.tensor_scalar(o, a, a1, a2, op0, **kw), rd, wr)

    def stt(self, eng, out, in0, scalar, in1, op0, op1, accum_out=None):
        o, a, b = out.ap, in0.ap, in1.ap
        s = self._a(scalar)
        wr = [out]
        kw = {}
        if accum_out is not None:
            kw["accum_out"] = accum_out.ap
            wr.append(accum_out)
        return self.op(eng, lambda e: e.scalar_tensor_tensor(o, a, s, b, op0, op1, **kw),
                       [in0, scalar, in1], wr)

    def copy(self, eng, out, in_):
        o, i = out.ap, in_.ap
        if eng == "act":
            return self.op(eng, lambda e: e.copy(o, i), [in_], [out])
        return self.op(eng, lambda e: e.tensor_copy(o, i), [in_], [out])

    def memset(self, eng, out, val):
        o = out.ap
        return self.op(eng, lambda e: e.memset(o, val), [], [out])

    def recip(self, out, in_):
        o, i = out.ap, in_.ap
        return self.op("dve", lambda e: e.reciprocal(o, i), [in_], [out])

    def reduce(self, eng, out, in_, op, axis=AX.X):
        o, i = out.ap, in_.ap
        return self.op(eng, lambda e: e.tensor_reduce(o, i, axis, op), [in_], [out])

    def emit(self, final_events):
        nc = self.nc
        fin = {}
        for (k, v) in final_events:
            fin[k] = max(fin.get(k, 0), v)
        finw = self._emit_waits("sp", fin, False)
        self.ops["sp"].append((finw, None, None, 0))
        sems = self.sems
        ops = self.ops
        needed = {e: set() for e in ENGS}
        for name in ENGS:
            for (waits, fn, key, inc) in ops[name]:
                for (k, v) in waits:
                    if k in needed:
                        needed[k].add(v)
        cmap = {e: {v: i + 1 for i, v in enumerate(sorted(needed[e]))} for e in ENGS}
        self.n_signals = {e: len(cmap[e]) for e in ENGS}

        def replay(name, e):
            raw = 0
            for (waits, fn, key, inc) in ops[name]:
                for (k, v) in waits:
                    e.wait_ge(sems[k], cmap[k][v] if k in cmap else v)
                if fn is not None:
                    ins = fn(e)
                    if key in cmap:
                        raw += 1
                        if raw in cmap[key]:
                            ins.then_inc(sems[key], 1)
                    else:
                        ins.then_inc(sems[key], inc)

        with nc.Block() as block:
            @block.sync
            def _(e):
                replay("sp", e)

            @block.scalar
            def _(e):
                replay("act", e)

            @block.vector
            def _(e):
                replay("dve", e)

            @block.gpsimd
            def _(e):
                replay("pool", e)

            @block.tensor
            def _(e):
                replay("pe", e)
        self.es.close()


D = 1024
DFF = 2816
NJ = DFF // 128
KC = D // 128
PT = 3792
NCB = 8
CBW = PT // NCB
EPS = 1e-6
G = 512
TPG = G // 128


def build_token_pass(ntok, stages, ffn_ids):
    nc = bass.Bass("TRN2", target_bir_lowering=False)
    P = Prog(nc)
    ngroups = ntok // G
    x_in = P.dram("x", [ntok, D], F32, kind="ExternalInput")
    ident_d = P.dram("ident", [128, 128], F32, kind="ExternalInput")
    xo = P.dram("xo", [ntok, D], F32, kind="ExternalOutput")
    dr = {}
    if "wout" in stages:
        dr["ycat"] = P.dram("ycat", [ntok, D], F32, kind="ExternalInput")
        dr["wo"] = P.dram("wo", [2, 128, KC, 512], F32, kind="ExternalInput")
        dr["grow_o"] = P.dram("grow_o", [128, D], F32, kind="ExternalInput")
    for tag in ffn_ids:
        dr["wg" + tag] = P.dram("wg" + tag, [NJ, 128, KC, 128], F32, kind="ExternalInput")
        dr["wu" + tag] = P.dram("wu" + tag, [NJ, 128, KC, 128], F32, kind="ExternalInput")
        dr["wd" + tag] = P.dram("wd" + tag, [NJ, 128, D], F32, kind="ExternalInput")
        dr["gcol" + tag] = P.dram("gcol" + tag, [128, KC], F32, kind="ExternalInput")
        dr["grow" + tag] = P.dram("grow" + tag, [128, D], F32, kind="ExternalInput")
    if "proj" in stages:
        dr["win"] = P.dram("win", [NCB, 128, KC, CBW], F32, kind="ExternalInput")
        dr["gcol_p"] = P.dram("gcol_p", [128, KC], F32, kind="ExternalInput")
        z_out = P.dram("z", [ntok, PT], F32, kind="ExternalOutput")

    identf = P.sbuf("identf", [128, 128], F32)
    identb = P.sbuf("identb", [128, 128], BF16)
    xt = [P.sbuf("xt%d" % i, [128, D], F32) for i in range(TPG)]
    hT = P.sbuf("hT", [128, KC, G], BF16)
    actT = P.sbuf("actT", [128, NJ, G], BF16)
    wdb = P.sbuf("wdb", [128, NJ, D], BF16)
    stg = [P.sbuf("stg%d" % i, [128, D], F32) for i in range(6)]
    wgb = [P.sbuf("wgb%d" % i, [128, KC, 128], BF16) for i in range(2)]
    wub = [P.sbuf("wub%d" % i, [128, KC, 128], BF16) for i in range(2)]
    xs = [P.sbuf("xs%d" % i, [128, D], BF16) for i in range(2)]
    junk = P.sbuf("junk", [128, D], F32)
    ysb = P.sbuf("ysb", [128, D], F32)
    sg = [P.sbuf("sg%d" % i, [128, 512], BF16) for i in range(2)]
    wpb = [P.sbuf("wpb%d" % i, [128, KC, CBW], BF16) for i in range(2)] if "proj" in stages else None
    grow_sb = {}
    gcol_sb = {}
    small = P.sbuf("small", [128, 64], F32)
    ps = [P.psum("ps%d" % i, [128, 512], F32) for i in range(8)]
    outev = []

    P.dma("sp", identf.v(), ident_d.v())
    P.copy("dve", identb.v(), identf.v())
    for k in list(dr.keys()):
        if k.startswith("grow"):
            grow_sb[k] = P.sbuf(k + "_sb", [128, D], F32)
            P.dma("sp", grow_sb[k].v(), dr[k].v())
        if k.startswith("gcol"):
            gcol_sb[k] = P.sbuf(k + "_sb", [128, KC], F32)
            P.dma("sp", gcol_sb[k].v(), dr[k].v())

    state = {"stg": 0, "sm": 0, "cast": 0}

    def next_stg():
        s = stg[state["stg"] % len(stg)]
        state["stg"] += 1
        return s

    def sm_col():
        c = state["sm"] % 64
        state["sm"] += 1
        return small[:, c:c + 1]

    cast_engs = ("pool", "dve", "pool", "act")

    def cast(out, in_):
        e = cast_engs[state["cast"] % len(cast_engs)]
        state["cast"] += 1
        P.copy(e, out, in_)

    def rstd_of(src):
        ss = sm_col()
        P.memset("dve", ss, 0.0)
        P.act(junk.v(), src, AF.Square, accum_out=ss)
        r = sm_col()
        P.ts("dve", r, ss, 1.0 / D, ALU.mult, EPS, ALU.add)
        P.act(r, r, AF.Sqrt)
        P.recip(r, r)
        return r

    def to_hT(src_tiles, gcol, normalize):
        for t in range(TPG):
            src = src_tiles[t]
            xb = xs[t % 2]
            if normalize:
                r = rstd_of(src.v())
                P.act(xb.v(), src.v(), AF.Copy, scale=r)
            else:
                P.copy("act", xb.v(), src.v())
            tp = ps[6 + t % 2].v().bitcast(BF16)
            for kc in range(KC):
                P.transpose(tp[:, kc * 128:(kc + 1) * 128], xb[:, kc * 128:(kc + 1) * 128], identb.v())
            for kc in range(KC):
                dst = hT[:, kc, t * 128:(t + 1) * 128]
                if gcol is not None:
                    P.ts("dve", dst, tp[:, kc * 128:(kc + 1) * 128], gcol[:, kc:kc + 1], ALU.mult)
                else:
                    P.copy("dve", dst, tp[:, kc * 128:(kc + 1) * 128])

    def post_norm_residual(yps0, yps1, grow, xtile, half):
        P.copy("act", ysb[:, 0:512], yps0.v())
        P.copy("act", ysb[:, 512:1024], yps1.v())
        r = rstd_of(ysb.v())
        if half:
            P.ts("dve", r, r, 0.5, ALU.mult)
        P.stt("dve", ysb.v(), ysb.v(), r, grow.v(), ALU.mult, ALU.mult)
        P.tt("pool", xtile.v(), xtile.v(), ysb.v(), ALU.add)

    def ffn(tag):
        to_hT(xt, gcol_sb["gcol" + tag], True)
        for j in range(NJ):
            s1, s2, s3 = next_stg(), next_stg(), next_stg()
            P.dma("sp", s1.v(), dr["wg" + tag][j].rearrange("p k c -> p (k c)"))
            P.dma("sp", s2.v(), dr["wu" + tag][j].rearrange("p k c -> p (k c)"))
            P.dma("sp", s3.v(), dr["wd" + tag][j])
            wg, wu = wgb[j % 2], wub[j % 2]
            cast(wg.v().rearrange("p k c -> p (k c)"), s1.v())
            cast(wu.v().rearrange("p k c -> p (k c)"), s2.v())
            cast(wdb[:, j, :], s3.v())
            pg, pu = ps[(j % 2) * 2], ps[(j % 2) * 2 + 1]
            for kc in range(KC):
                P.mm(pg.v(), wg[:, kc, :], hT[:, kc, :], start=(kc == 0), stop=(kc == KC - 1))
            for kc in range(KC):
                P.mm(pu.v(), wu[:, kc, :], hT[:, kc, :], start=(kc == 0), stop=(kc == KC - 1))
            sgt = sg[j % 2]
            P.act(sgt.v(), pg.v(), AF.Silu)
            P.tt("dve", actT[:, j, :], pu.v(), sgt.v(), ALU.mult)
        for t in range(TPG):
            y0, y1 = ps[4], ps[5]
            for n, yp in enumerate((y0, y1)):
                for j in range(NJ):
                    P.mm(yp.v(), actT[:, j, t * 128:(t + 1) * 128], wdb[:, j, n * 512:(n + 1) * 512],
                         start=(j == 0), stop=(j == NJ - 1))
            post_norm_residual(y0, y1, grow_sb["grow" + tag], xt[t], True)

    def wout(g0):
        yc = []
        for t in range(TPG):
            s = next_stg()
            P.dma("sp", s.v(), dr["ycat"][g0 + t * 128:g0 + (t + 1) * 128, :])
            yc.append(s)
        to_hT(yc, None, False)
        wob = []
        for nb in range(2):
            s1, s2 = next_stg(), next_stg()
            P.dma("sp", s1.v(), dr["wo"][nb, :, 0:2, :].rearrange("p k c -> p (k c)"))
            P.dma("sp", s2.v(), dr["wo"][nb, :, 2:4, :].rearrange("p k c -> p (k c)"))
            s3, s4 = next_stg(), next_stg()
            P.dma("sp", s3.v(), dr["wo"][nb, :, 4:6, :].rearrange("p k c -> p (k c)"))
            P.dma("sp", s4.v(), dr["wo"][nb, :, 6:8, :].rearrange("p k c -> p (k c)"))
            for q, s in enumerate((s1, s2, s3, s4)):
                cast(wdb[:, nb * 4 + q, :], s.v())
        for t in range(TPG):
            y0, y1 = ps[4], ps[5]
            for nb, yp in enumerate((y0, y1)):
                for kc in range(KC):
                    w = wdb[:, nb * 4 + kc // 2, (kc % 2) * 512:(kc % 2) * 512 + 512]
                    P.mm(yp.v(), hT[:, kc, t * 128:(t + 1) * 128], w, start=(kc == 0), stop=(kc == KC - 1))
            post_norm_residual(y0, y1, grow_sb["grow_o"], xt[t], False)

    def proj(g0):
        to_hT(xt, gcol_sb["gcol_p"], True)
        for cb in range(NCB):
            wp = wpb[cb % 2]
            for q in range(4):
                s = next_stg()
                P.dma("sp", s[:, 0:2 * CBW], dr["win"][cb, :, 2 * q:2 * q + 2, :].rearrange("p k c -> p (k c)"))
                cast(wp[:, 2 * q:2 * q + 2, :].rearrange("p k c -> p (k c)"), s[:, 0:2 * CBW])
            for t in range(TPG):
                zp = ps[(t % 2) * 2]
                for kc in range(KC):
                    w = wp[:, kc, :]
                    P.mm(zp[:, 0:CBW], hT[:, kc, t * 128:(t + 1) * 128], w, start=(kc == 0), stop=(kc == KC - 1))
                s = next_stg()
                P.copy("act", s[:, 0:CBW], zp[:, 0:CBW])
                outev.append(P.dma("pool", z_out[g0 + t * 128:g0 + (t + 1) * 128, cb * CBW:(cb + 1) * CBW],
                                   s[:, 0:CBW]))

    for gi in range(ngroups):
        g0 = gi * G
        for t in range(TPG):
            P.dma("sp", xt[t].v(), x_in[g0 + t * 128:g0 + (t + 1) * 128, :])
        for st in stages:
            if st == "wout":
                wout(g0)
            elif st.startswith("ffn"):
                ffn(st[3:])
            elif st == "proj":
                proj(g0)
        for t in range(TPG):
            outev.append(P.dma("pool", xo[g0 + t * 128:g0 + (t + 1) * 128, :], xt[t].v()))
    P.emit(outev)
    return nc


def lay_ffn(w_gate, w_up, w_down, g_pre, g_post):
    wg = np.ascontiguousarray(w_gate.reshape(KC, 128, NJ, 128).transpose(2, 1, 0, 3))
    wu = np.ascontiguousarray(w_up.reshape(KC, 128, NJ, 128).transpose(2, 1, 0, 3))
    wd = np.ascontiguousarray(w_down.reshape(NJ, 128, D))
    gcol = np.ascontiguousarray(g_pre.reshape(KC, 128).T)
    grow = np.ascontiguousarray(np.broadcast_to(g_post[None, :], (128, D)))
    return wg, wu, wd, gcol, grow


def lay_win(w_in, g_pre):
    win = np.ascontiguousarray(w_in.reshape(KC, 128, NCB, CBW).transpose(2, 1, 0, 3))
    gcol = np.ascontiguousarray(g_pre.reshape(KC, 128).T)
    return win, gcol


def lay_wout(w_out, g_post):
    wo = np.ascontiguousarray(w_out.reshape(KC, 128, 2, 512).transpose(2, 1, 0, 3))
    grow = np.ascontiguousarray(np.broadcast_to(g_post[None, :], (128, D)))
    return wo, grow


CH = 128


def build_mixers(S, which=("conv", "ret", "mlstm", "rwkv"), stop=99):
    nc = bass.Bass("TRN2", target_bir_lowering=False)
    P = Prog(nc)
    NCH = S // CH
    outev = []
    din = lambda name, shape: P.dram(name, shape, F32, kind="ExternalInput")
    dout = lambda name, shape: P.dram(name, shape, F32, kind="ExternalOutput")
    ps = [P.psum("ps%d" % i, [128, 512], F32) for i in range(8)]
    cst = din("cst", [128, 8, 128])
    cstc = din("cstc", [128, 4])
    C = P.sbuf("C", [128, 8, 128], F32)
    Cc = P.sbuf("Cc", [128, 4], F32)
    P.dma("sp", C.v(), cst.v())
    P.dma("sp", Cc.v(), cstc.v())
    identf = C[:, 0, :]
    identb_t = P.sbuf("identb", [128, 128], BF16)
    P.copy("dve", identb_t.v(), identf)
    rr = {"i": 0}

    def EW():
        rr["i"] += 1
        return ("dve", "pool")[rr["i"] % 2]

    def neg_logsig_setup(dst, src):
        P.act(dst, src, AF.Exp, scale=-1.0)
        P.ts("dve", dst, dst, 1.0, ALU.add)
        P.act(dst, dst, AF.Ln)
        P.ts("dve", dst, dst, -1.0, ALU.mult)

    if "conv" in which:
        cA = din("cA", [3, 128, S])
        cw = din("cw", [128, 4])
        ycv = dout("y_conv", [128, S])
        cwt = P.sbuf("cwt", [128, 4], F32)
        P.dma("sp", cwt.v(), cw.v())
        BW = min(S, 1024)
        ub = P.sbuf("cv_u", [128, BW + 2], F32)
        tb = P.sbuf("cv_t", [128, BW + 2], F32)
        bb = P.sbuf("cv_b", [128, BW], F32)
        yb = P.sbuf("cv_y", [128, BW], F32)
        for b0 in range(0, S, BW):
            lo = max(b0 - 1, 0)
            hi = min(b0 + BW + 1, S)
            o = lo - (b0 - 1)
            n = hi - lo
            if b0 == 0:
                P.memset("pool", ub[:, 0:1], 0.0)
            if b0 + BW >= S:
                P.memset("pool", ub[:, BW + 1:BW + 2], 0.0)
            P.dma("sp", ub[:, o:o + n], cA[1, :, lo:hi])
            P.dma("sp", tb[:, o:o + n], cA[2, :, lo:hi])
            P.dma("sp", bb.v(), cA[0, :, b0:b0 + BW])
            P.tt("dve", ub[:, o:o + n], ub[:, o:o + n], tb[:, o:o + n], ALU.mult)
            P.ts("dve", yb.v(), ub[:, 1:BW + 1], cwt[:, 1:2], ALU.mult, cwt[:, 3:4], ALU.add)
            P.stt("dve", yb.v(), ub[:, 0:BW], cwt[:, 0:1], yb.v(), ALU.mult, ALU.add)
            P.stt("dve", yb.v(), ub[:, 2:BW + 2], cwt[:, 2:3], yb.v(), ALU.mult, ALU.add)
            P.tt("pool", yb.v(), yb.v(), bb.v(), ALU.mult)
            outev.append(P.dma("pool", ycv[:, b0:b0 + BW], yb.v()))

    shA = [P.sbuf("shA%d" % d, [128, NCH, 72], F32) for d in range(2)]
    shB = [P.sbuf("shB%d" % d, [128, NCH, 72], BF16) for d in range(2)]
    shV = P.sbuf("shV", [128, NCH, 144], BF16)
    f32t = [P.sbuf("f%d" % i, [128, 128], F32) for i in range(16)]
    b16t = [P.sbuf("b%d" % i, [128, 130], BF16) for i in range(12)]
    sm = P.sbuf("sm", [128, 32], F32)

    def head_rms_scale(o_sb, ncols_per_head, eps, nheads=2):
        ss = sm[:, 0:nheads]
        P.memset("dve", ss, 0.0)
        for h in range(nheads):
            P.act(f32t[15][:, 0:ncols_per_head], o_sb[:, h * ncols_per_head:(h + 1) * ncols_per_head],
                  AF.Square, accum_out=sm[:, h:h + 1])
        r = sm[:, 4:4 + nheads]
        P.ts("dve", r, ss, 1.0 / ncols_per_head, ALU.mult, eps, ALU.add)
        P.act(r, r, AF.Sqrt)
        P.recip(r, r)
        return r

    if "ret" in which:
        rT = din("rT", [NCH, 128, 6, CH])
        rtok = din("rtok", [NCH, 128, 6, 128])
        ldT = [P.sbuf("r_ld%d" % i, [128, 6, 128], F32) for i in range(2)]
        rdl = din("rdl", [128, 6])
        yret = dout("y_ret", [S, 128])
        lg = P.sbuf("r_lg", [128, 6], F32)
        P.dma("sp", lg.v(), rdl.v())
        neg_logsig_setup(lg.v(), lg.v())
        MT = P.sbuf("r_MT", [128, 2, 128], F32)
        for h in range(2):
            P.act(MT[:, h, :], C[:, 1, :], AF.Exp, scale=lg[:, h:h + 1])
            P.act(f32t[0].v(), C[:, 2, :], AF.Exp, scale=lg[:, 2 + h:3 + h])
            P.tt("dve", MT[:, h, :], MT[:, h, :], f32t[0].v(), ALU.add)
        qdT = P.sbuf("r_qdT", [128, 2, 128], F32)
        P.act(qdT[:, 0, :], C[:, 3, :], AF.Exp, scale=lg[:, 4:5])
        P.act(qdT[:, 1, :], C[:, 4, :], AF.Exp, scale=lg[:, 5:6])
        P.ts("dve", qdT.v(), qdT.v(), 0.125, ALU.mult)
        kdec = P.sbuf("r_kdec", [128, 4], F32)
        for j in range(4):
            P.act(kdec[:, j:j + 1], lg[:, j:j + 1], AF.Exp, scale=Cc[:, j // 2:j // 2 + 1])
        cdec = P.sbuf("r_cdec", [128, 2], F32)
        P.act(cdec.v(), lg[:, 4:6], AF.Exp, scale=128.0)
        Uall, Sst, vall = shA, shB, shV
        Sc = P.sbuf("r_Sc", [128, 64], F32)
        for c in range(NCH if stop >= 1 else 0):
            t0 = c * CH
            L = ldT[c % 2]
            P.dma("sp", L.v(), rtok[c])
            k, ks, kp = f32t[0], f32t[1], f32t[5]
            P.tt("dve", k.v(), L[:, 0, :], L[:, 4, :], ALU.mult)
            P.tt("pool", ks.v(), L[:, 1, :], L[:, 5, :], ALU.mult)
            P.tt("dve", kp.v(), k.v(), ks.v(), ALU.add)
            P.copy("pool", vall[:, c, 0:128], L[:, 2, :])
            for d in range(2):
                kd = b16t[d]
                for h in range(2):
                    P.ts("dve", kd[:, h * 64:(h + 1) * 64], kp[:, h * 64:(h + 1) * 64],
                         kdec[:, 2 * d + h:2 * d + h + 1], ALU.mult)
                pu = ps[d]
                P.mm(pu[:, 0:128], kd[:, 0:128], vall[:, c, 0:128])
                P.copy("act", Uall[d][0:64, c, 0:64], pu[0:64, 0:64])
                P.copy("act", Uall[d][64:128, c, 0:64], pu[64:128, 64:128])
        for d in range(2 if stop >= 2 else 0):
            P.memset("dve", Sc.v(), 0.0)
            order = range(NCH) if d == 0 else range(NCH - 1, -1, -1)
            for c in order:
                P.copy("dve", Sst[d][:, c, 0:64], Sc.v())
                P.stt("dve", Sc.v(), Sc.v(), cdec[:, d:d + 1], Uall[d][:, c, 0:64], ALU.mult, ALU.add)
        if stop < 3:
            dbg = dout("dbg", [128, 128])
            outev.append(P.dma("pool", dbg.v(), MT[:, 0, :]))
        for c in range(NCH if stop >= 3 else 0):
            t0 = c * CH
            L = ldT[c % 2]
            P.dma("sp", L.v(), rT[c])
            q, qs, k, ks = (f32t[i] for i in range(4))
            g = f32t[6]
            P.dma("sp", g.v(), rtok[c, :, 3, :])
            P.tt("dve", q.v(), L[:, 0, :], L[:, 4, :], ALU.mult)
            P.tt("pool", qs.v(), L[:, 1, :], L[:, 5, :], ALU.mult)
            P.tt("dve", q.v(), q.v(), qs.v(), ALU.add)
            P.tt("pool", k.v(), L[:, 2, :], L[:, 4, :], ALU.mult)
            P.tt("pool", ks.v(), L[:, 3, :], L[:, 5, :], ALU.mult)
            qb, qdf, qdb, kb = b16t[2], b16t[3], b16t[4], b16t[5]
            P.tt("dve", kb[:, 0:128], k.v(), ks.v(), ALU.add)
            P.ts("dve", qb[:, 0:128], q.v(), 0.125, ALU.mult)
            P.tt("dve", qdf[:, 0:128], q.v(), qdT[:, 0, :], ALU.mult)
            P.tt("pool", qdb[:, 0:128], q.v(), qdT[:, 1, :], ALU.mult)
            if stop == 3:
                outev.append(P.dma("pool", yret[t0:t0 + CH, :], q.v()))
                continue
            scs = (ps[2], ps[4])
            pos = (ps[3], ps[5])
            for h in range(2):
                hs = slice(h * 64, (h + 1) * 64)
                P.mm(scs[h][:, 0:128], kb[hs, 0:128], qb[hs, 0:128])
            if stop == 4:
                P.copy("act", f32t[7].v(), scs[0][:, 0:128])
                outev.append(P.dma("pool", yret[t0:t0 + CH, :], f32t[7].v()))
                continue
            for h in range(2):
                hs = slice(h * 64, (h + 1) * 64)
                pt = b16t[6 + h]
                P.tt("dve", pt[:, 0:128], scs[h][:, 0:128], MT[:, h, :], ALU.mult)
                P.mm(pos[h][:, 0:64], pt[:, 0:128], vall[:, c, hs], start=True, stop=False)
                P.mm(pos[h][:, 0:64], qdf[hs, 0:128], Sst[0][hs, c, 0:64], start=False, stop=False)
                P.mm(pos[h][:, 0:64], qdb[hs, 0:128], Sst[1][hs, c, 0:64], start=False, stop=True)
            o = f32t[7]
            for h in range(2):
                P.copy("act", o[:, h * 64:(h + 1) * 64], pos[h][:, 0:64])
            if stop == 5:
                outev.append(P.dma("pool", yret[t0:t0 + CH, :], o.v()))
                continue
            r = head_rms_scale(o, 64, EPS)
            P.act(g.v(), g.v(), AF.Silu)
            y = f32t[8]
            for h in range(2):
                hs = slice(h * 64, (h + 1) * 64)
                P.stt("dve", y[:, hs], o[:, hs], r[:, h:h + 1], g[:, hs], ALU.mult, ALU.mult)
            outev.append(P.dma("pool", yret[t0:t0 + CH, :], y.v()))

    if "rwkv" in which:
        wz = din("wz", [NCH, 128, 3, 576])
        wmu = din("wmu", [128, 576])
        wpar = din("wpar", [128, 9, 128])
        wlow = din("wlow", [64, 3, 128])
        wsc = P.dram("wkv_scr", [S, 128], F32, kind="ExternalOutput")
        bsall = P.sbuf("w_bsall", [128, NCH, 2], F32)
        yrw = dout("y_rwkv", [S, 128])
        mu_t = P.sbuf("w_mu", [128, 576], F32)
        par = P.sbuf("w_par", [128, 9, 128], F32)
        low = P.sbuf("w_low", [64, 3, 128], F32)
        P.dma("sp", mu_t.v(), wmu.v())
        P.dma("sp", par.v(), wpar.v())
        P.dma("sp", low.v(), wlow.v())
        W = {}
        for nm in ("oma", "SU", "SL", "nTU", "nTL", "zm4", "ld", "a", "gate", "kk", "kd", "beta", "tmp", "tmp2",
                   "eg", "eng", "egp", "egs", "kap", "bet", "kt", "rt", "Kh", "nBh", "nA2T", "MkT", "A1T",
                   "X", "XT", "RH", "WU", "yo", "fin"):
            W[nm] = P.sbuf("w_" + nm, [128, 128], F32)
        P.ts("dve", W["oma"].v(), par[:, 5, :], -1.0, ALU.mult, 1.0, ALU.add)
        P.tt("dve", W["SU"].v(), C[:, 5, :], identf, ALU.subtract)
        P.tt("dve", W["SL"].v(), C[:, 6, :], identf, ALU.subtract)
        P.ts("dve", W["nTU"].v(), C[:, 5, :], -1.0, ALU.mult)
        P.ts("dve", W["nTL"].v(), C[:, 6, :], -1.0, ALU.mult)
        Zb = [P.sbuf("w_Z%d" % i, [128, 3, 576], F32) for i in range(2)]
        t576 = P.sbuf("w_t576", [128, 576], F32)
        zm = P.sbuf("w_zm", [128, 576], F32)
        lowT = P.sbuf("w_lowT", [64, 3, 128], F32)
        FT = [P.sbuf("w_FT%d" % h, [64, 4, 128], F32) for h in range(2)]
        Asq = [[P.sbuf("w_A%d_%d" % (h, i), [128, 2, 128], F32) for i in range(3)] for h in range(2)]
        St = [P.sbuf("w_St%d" % h, [128, 64], F32) for h in range(2)]
        QeT = P.sbuf("w_QeT", [128, 128], F32)
        GT = P.sbuf("w_GT", [128, 64], F32)
        P.memset("pool", QeT.v(), 0.0)
        P.memset("pool", GT.v(), 0.0)
        gam = P.sbuf("w_gam", [64, 16], F32)
        wsm = P.sbuf("w_sm", [128, 16], F32)
        ysb = P.sbuf("w_ysb", [128, 128], F32)
        prev = P.sbuf("w_prev", [128, 128], F32)
        bst = P.sbuf("w_bst", [128, 16], F32)
        bk = {"i": 0}

        def bank():
            bk["i"] += 1
            return ps[bk["i"] % 7]

        r_, k_, v_ = zm[:, 0:128], zm[:, 128:256], zm[:, 256:384]
        ones8 = C[:, 7, 0:8]

        class _Stop(Exception):
            pass

        def chk(level, view):
            if stop == level:
                P.copy("act", W["fin"][0:view.shape[0], 0:view.shape[1]], view)
                outev.append(P.dma("pool", yrw[0:128, :], W["fin"].v()))
                raise _Stop()

        try:
            for d in range({100: 0, 10: 1}.get(stop, 2)):
                tri = C[:, 5 + d, :]
                sexcl = (W["SL"] if d == 0 else W["SU"]).v()
                mstrict = (W["SU"] if d == 0 else W["SL"]).v()
                mstrictT = (W["SL"] if d == 0 else W["SU"]).v()
                mincl = C[:, 5 + d, :]
                nmincl = (W["nTU"] if d == 0 else W["nTL"]).v()
                for h in range(2):
                    P.memset("dve", St[h].v(), 0.0)
                order = range(NCH) if d == 0 else range(NCH - 1, -1, -1)
                for ci, c in enumerate(order):
                    if bk.get("stopped"):
                        break
                    t0 = c * CH
                    Zt = Zb[ci % 2]
                    P.dma("sp", Zt.v(), wz[c])
                    if d == 1:
                        P.dma("sp", prev.v(), wsc[t0:t0 + CH, :])
                    P.tt("pool", t576.v(), Zt[:, 1, :], Zt[:, 2, :], ALU.add)
                    P.ts("dve", t576.v(), t576.v(), 0.5, ALU.mult)
                    P.tt("pool", t576.v(), t576.v(), Zt[:, 0, :], ALU.subtract)
                    P.tt("dve", t576.v(), t576.v(), mu_t.v(), ALU.mult)
                    P.tt("pool", zm.v(), Zt[:, 0, :], t576.v(), ALU.add)
                    chk(1, zm[:, 0:128])
                    pl = bank()
                    for i in range(3):
                        P.transpose(pl[0:64, i * 128:(i + 1) * 128], zm[:, 384 + 64 * i:448 + 64 * i], identf)
                    P.act(lowT[:, 0, :], pl[0:64, 0:128], AF.Tanh)
                    P.copy("dve", lowT[:, 1, :], pl[0:64, 128:256])
                    P.act(lowT[:, 2, :], pl[0:64, 256:384], AF.Sigmoid)
                    chk(2, lowT[:, 0, :])
                    rs = slice(d * 32, (d + 1) * 32)
                    pxw, pxa, pg = bank(), bank(), bank()
                    P.mm(pxw[:, 0:128], lowT[rs, 0, :], low[rs, 0, :])
                    P.mm(pxa[:, 0:128], lowT[rs, 1, :], low[rs, 1, :])
                    P.mm(pg[:, 0:128], lowT[0:64, 2, :], low[0:64, 2, :])
                    ld, a = W["ld"].v(), W["a"].v()
                    P.tt("dve", ld, pxw[:, 0:128], par[:, d, :], ALU.add)
                    P.act(ld, ld, AF.Sigmoid)
                    P.ts("dve", ld, ld, -0.6065306597126334, ALU.mult)
                    P.tt("dve", a, pxa[:, 0:128], par[:, 2 + d, :], ALU.add)
                    P.act(a, a, AF.Sigmoid)
                    P.copy("act", W["gate"].v(), pg[:, 0:128])
                    chk(3, W['ld'].v())
                    kk, kd, beta, tmp, tmp2 = (W[n].v() for n in ("kk", "kd", "beta", "tmp", "tmp2"))
                    P.tt("pool", kk, k_, par[:, 4, :], ALU.mult)
                    ss2 = wsm[:, 0:2]
                    P.memset("dve", ss2, 0.0)
                    for h in range(2):
                        P.act(f32t[15][:, 0:64], W["kk"][:, h * 64:(h + 1) * 64], AF.Square, accum_out=wsm[:, h:h + 1])
                    P.act(ss2, ss2, AF.Sqrt)
                    P.ts("dve", ss2, ss2, 1e-12, ALU.max)
                    P.recip(ss2, ss2)
                    for h in range(2):
                        hs = slice(h * 64, (h + 1) * 64)
                        P.ts("dve", W["kk"][:, hs], W["kk"][:, hs], wsm[:, h:h + 1], ALU.mult)
                    P.tt("pool", tmp, a, par[:, 5, :], ALU.mult)
                    P.tt("pool", tmp, tmp, W["oma"].v(), ALU.add)
                    P.tt("pool", kd, k_, tmp, ALU.mult)
                    P.tt("pool", beta, a, kk, ALU.mult)
                    P.tt("pool", tmp, r_, par[:, 6, :], ALU.mult)
                    P.tt("pool", tmp, tmp, kd, ALU.mult)
                    bs = bst[:, 0:2]
                    P.reduce("dve", bs, W["tmp"].v().rearrange("p (h c) -> p h c", h=2), ALU.add)
                    chk(4, W['kd'].v())
                    pg1, pg2, pgt = bank(), bank(), bank()
                    P.mm(pg1[:, 0:128], tri, ld)
                    P.mm(pg2[:, 0:128], sexcl, ld)
                    P.mm(pgt[0:64, 0:8], W["ld"][:, 0:64], ones8)
                    P.mm(pgt[0:64, 8:16], W["ld"][:, 64:128], ones8)
                    P.act(gam.v(), pgt[0:64, 0:16], AF.Exp)
                    eg, eng, egp, egs = (W[n].v() for n in ("eg", "eng", "egp", "egs"))
                    P.act(eg, pg1[:, 0:128], AF.Exp)
                    P.act(eng, pg1[:, 0:128], AF.Exp, scale=-1.0)
                    P.tt("dve", tmp2, pg1[:, 0:128], ld, ALU.subtract)
                    P.act(egp, tmp2, AF.Exp)
                    P.act(egs, pg2[:, 0:128], AF.Exp)
                    P.tt("pool", W["kap"].v(), kk, egp, ALU.mult)
                    P.tt("pool", W["bet"].v(), beta, eng, ALU.mult)
                    P.tt("pool", W["kt"].v(), kd, eng, ALU.mult)
                    P.tt("pool", W["rt"].v(), r_, eg, ALU.mult)
                    P.tt("pool", W["Kh"].v(), kd, egs, ALU.mult)
                    P.tt("pool", tmp, beta, egs, ALU.mult)
                    P.ts("dve", W["nBh"].v(), tmp, -1.0, ALU.mult)
                    chk(5, W['kap'].v())
                    py = ps[7]
                    for h in range(2):
                        hs = slice(h * 64, (h + 1) * 64)
                        pf = bank()
                        for i, nm in enumerate(("kap", "rt", "bet", "kt")):
                            P.transpose(pf[0:64, i * 128:(i + 1) * 128], W[nm][:, hs], identf)
                        FTh = FT[h]
                        P.copy("act", FTh.v().rearrange("p a b -> p (a b)"), pf[0:64, 0:512])
                        pa, pb = bank(), bank()
                        P.mm(pa[:, 0:256], FTh[:, 2, :], FTh[:, 0:2, :].rearrange("p a b -> p (a b)"))
                        P.mm(pb[:, 0:256], FTh[:, 3, :], FTh[:, 0:2, :].rearrange("p a b -> p (a b)"))
                        P.mm(pa[:, 256:384], FTh[:, 0, :], FTh[:, 2, :])
                        chk(6, FTh[:, 0, :])
                        A0 = Asq[h][2]
                        P.tt("dve", A0[:, 0, :], pa[:, 0:128], mstrict, ALU.mult)
                        P.tt("dve", A0[:, 1, :], pa[:, 256:384], mstrictT, ALU.mult)
                        P.tt("dve", W["nA2T"].v(), pa[:, 128:256], nmincl, ALU.mult)
                        P.tt("dve", W["MkT"].v(), pb[:, 0:128], mstrict, ALU.mult)
                        P.tt("dve", W["A1T"].v(), pb[:, 128:256], mincl, ALU.mult)
                        chk(7, W['A1T'].v())
                        X, XT = W["X"].v(), W["XT"].v()
                        P.tt("pool", X, identf, A0[:, 0, :], ALU.subtract)
                        P.tt("pool", XT, identf, A0[:, 1, :], ALU.subtract)
                        cur = A0
                        for lvl in range(6):
                            pq = bank()
                            P.mm(pq[:, 0:128], cur[:, 1, :], cur[:, 0, :])
                            P.mm(pq[:, 128:256], cur[:, 0, :], cur[:, 1, :])
                            nxt = Asq[h][lvl % 2]
                            P.copy("act", nxt.v().rearrange("p a b -> p (a b)"), pq[:, 0:256])
                            pr = bank()
                            P.mm(pr[:, 0:128], XT, nxt[:, 0, :])
                            if lvl < 5:
                                P.mm(pr[:, 128:256], nxt[:, 0, :], XT)
                            P.tt("dve", X, pr[:, 0:128], X, ALU.add)
                            if lvl < 5:
                                P.tt("dve", XT, pr[:, 128:256], XT, ALU.add)
                            cur = nxt
                        chk(8, W['X'].v())
                        pm = bank()
                        P.mm(pm[:, 0:64], W["MkT"].v(), v_[:, hs])
                        P.copy("act", W["RH"][:, 64:128], pm[:, 0:64])
                        P.copy("pool", W["RH"][:, 0:64], W["kap"][:, hs])
                        chk(12, W['RH'].v())
                        pw = bank()
                        P.mm(pw[:, 0:128], X, W["RH"].v())
                        P.copy("act", W["WU"].v(), pw[:, 0:128])
                        chk(13, W['WU'].v())
                        pqe = bank()
                        P.mm(pqe[0:64, 0:128], W["WU"][:, 0:64], W["nA2T"].v())
                        P.tt("dve", QeT[0:64, :], pqe[0:64, 0:128], FTh[:, 1, :], ALU.add)
                        chk(14, QeT.v())
                        pgh = bank()
                        P.mm(pgh[0:64, 0:64], W["WU"][:, 0:64], W["nBh"][:, hs])
                        P.copy("act", GT[0:64, :], pgh[0:64, 0:64])
                        chk(15, GT.v())
                        P.mm(py[:, hs], W["A1T"].v(), v_[:, hs], start=True, stop=False)
                        P.mm(py[:, hs], W["nA2T"].v(), W["WU"][:, 64:128], start=False, stop=False)
                        P.mm(py[:, hs], QeT.v(), St[h].v(), start=False, stop=True)
                        chk(16, py[:, 0:64])
                        ph = bank()
                        P.mm(ph[0:64, 0:64], W["Kh"][:, hs], v_[:, hs], start=True, stop=False)
                        P.mm(ph[0:64, 0:64], W["nBh"][:, hs], W["WU"][:, 64:128], start=False, stop=False)
                        P.mm(ph[0:64, 0:64], GT.v(), St[h].v(), start=False, stop=True)
                        P.stt("dve", St[h][0:64, :], St[h][0:64, :], gam[:, 8 * h:8 * h + 1], ph[0:64, 0:64], ALU.mult, ALU.add)
                        chk(17, St[h].v())
                    chk(9, py[:, 0:128])
                    if d == 0:
                        P.copy("act", ysb.v(), py[:, 0:128])
                        P.copy("dve", bsall[:, c, :], bs)
                        P.dma("sp", wsc[t0:t0 + CH, :], ysb.v())
                        chk(11, ysb[:, 0:128])
                    else:
                        yo, fin = W["yo"].v(), W["fin"].v()
                        P.tt("dve", yo, py[:, 0:128], prev[:, 0:128], ALU.add)
                        bsum = wsm[:, 2:4]
                        P.tt("dve", bsum, bs, bsall[:, c, :], ALU.add)
                        mean = wsm[:, 4:6]
                        P.reduce("dve", mean, W["yo"].v().rearrange("p (h c) -> p h c", h=2), ALU.add)
                        P.ts("dve", mean, mean, 1.0 / 64, ALU.mult)
                        for h in range(2):
                            hs = slice(h * 64, (h + 1) * 64)
                            P.ts("dve", W["yo"][:, hs], W["yo"][:, hs], wsm[:, 4 + h:5 + h], ALU.subtract)
                        var = wsm[:, 6:8]
                        P.memset("dve", var, 0.0)
                        for h in range(2):
                            P.act(f32t[15][:, 0:64], W["yo"][:, h * 64:(h + 1) * 64], AF.Square, accum_out=wsm[:, 6 + h:7 + h])
                        P.ts("dve", var, var, 1.0 / 64, ALU.mult, 64e-5, ALU.add)
                        P.act(var, var, AF.Sqrt)
                        P.recip(var, var)
                        for h in range(2):
                            hs = slice(h * 64, (h + 1) * 64)
                            P.ts("dve", W["fin"][:, hs], W["yo"][:, hs], wsm[:, 6 + h:7 + h], ALU.mult)
                        P.tt("pool", fin, fin, par[:, 7, :], ALU.mult)
                        P.tt("pool", fin, fin, par[:, 8, :], ALU.add)
                        for h in range(2):
                            hs = slice(h * 64, (h + 1) * 64)
                            P.stt("dve", W["fin"][:, hs], v_[:, hs], wsm[:, 2 + h:3 + h], W["fin"][:, hs], ALU.mult, ALU.add)
                        P.tt("pool", fin, fin, W["gate"].v(), ALU.mult)
                        outev.append(P.dma("pool", yrw[t0:t0 + CH, :], fin))

        except _Stop:
            pass

    if "mlstm" in which:
        mT = din("mT", [NCH, 128, 2, CH])
        mtok = din("mtok", [NCH, 128, 3, 128])
        mgate = din("mgate", [NCH, 128, 8])
        mbias = din("mbias", [128, 8])
        mnw = din("mnw", [128, 128])
        ymls = dout("y_mlstm", [S, 128])
        VW = 72
        bias_t = P.sbuf("m_bias", [128, 16], F32)
        nw_t = P.sbuf("m_nw", [128, 128], F32)
        P.dma("sp", bias_t[:, 0:8], mbias.v())
        P.dma("sp", nw_t.v(), mnw.v())
        vaug = shV
        P.memset("pool", vaug.v(), 0.0)
        for h in range(2):
            P.memset("pool", vaug[:, :, h * VW + 64:h * VW + 65], 1.0)
        mU, mC = shA, shB
        EAll = P.sbuf("m_EA", [128, NCH, 8], F32)
        EB = P.sbuf("m_EB", [128, NCH, 2], F32)
        mL = [P.sbuf("m_ld%d" % i, [128, 3, 128], F32) for i in range(2)]
        mG = [P.sbuf("m_g%d" % i, [128, 16], F32) for i in range(2)]
        gl = P.sbuf("m_gl", [128, 32], F32)
        Cs = P.sbuf("m_Cs", [128, VW], F32)
        for c in range(NCH):
            L = mL[c % 2]
            Gt = mG[c % 2]
            P.dma("sp", L.v(), mtok[c])
            P.dma("sp", Gt[:, 0:8], mgate[c])
            li, lf = gl[:, 0:4], gl[:, 4:8]
            P.tt("dve", gl[:, 0:8], Gt[:, 0:8], bias_t[:, 0:8], ALU.add)
            neg_logsig_setup(lf, lf)
            pb = ps[0]
            P.mm(pb[:, 0:8], C[:, 5, :], gl[:, 0:8])
            P.mm(pb[:, 8:16], C[:, 6, :], gl[:, 0:8])
            P.mm(pb[:, 16:24], C[:, 7, :], gl[:, 0:8])
            bc, bt, a, w, ebt = gl[:, 8:12], gl[:, 12:16], gl[:, 16:20], gl[:, 20:24], gl[:, 24:28]
            P.copy("act", gl[:, 8:10], pb[:, 4:6])
            P.copy("act", gl[:, 10:12], pb[:, 14:16])
            P.copy("act", gl[:, 12:16], pb[:, 20:24])
            P.tt("dve", a, li, bc, ALU.subtract)
            P.act(EAll[:, c, 0:4], a, AF.Exp)
            P.act(EAll[:, c, 4:8], bc, AF.Exp)
            P.tt("dve", w, a, bt, ALU.add)
            P.act(w, w, AF.Exp)
            P.act(ebt, bt, AF.Exp)
            for d in range(2):
                P.copy("dve", EB[0:64, c, d:d + 1], ebt[0:64, 2 * d:2 * d + 1])
                P.copy("dve", EB[64:128, c, d:d + 1], ebt[64:128, 2 * d + 1:2 * d + 2])
            for h in range(2):
                P.copy("pool", vaug[:, c, h * VW:h * VW + 64], L[:, 1, h * 64:(h + 1) * 64])
            for d in range(2):
                kw = b16t[d]
                for h in range(2):
                    P.ts("dve", kw[:, h * 64:(h + 1) * 64], L[:, 0, h * 64:(h + 1) * 64],
                         w[:, 2 * d + h:2 * d + h + 1], ALU.mult)
                pu = ps[1 + d]
                P.mm(pu[:, 0:2 * VW], kw[:, 0:128], vaug[:, c, :])
                P.copy("act", mU[d][0:64, c, :], pu[0:64, 0:VW])
                P.copy("act", mU[d][64:128, c, :], pu[64:128, VW:2 * VW])
        for d in range(2):
            P.memset("dve", Cs.v(), 0.0)
            order = range(NCH) if d == 0 else range(NCH - 1, -1, -1)
            for c in order:
                P.copy("dve", mC[d][:, c, :], Cs.v())
                P.stt("dve", Cs.v(), Cs.v(), EB[:, c, d:d + 1], mU[d][:, c, :], ALU.mult, ALU.add)
        mLT = [P.sbuf("m_ldT%d" % i, [128, 2, CH], F32) for i in range(2)]
        na = P.sbuf("m_na", [128, 4, VW], F32)
        Hh = [P.sbuf("m_H%d" % d, [128, 128], F32) for d in range(2)]
        for c in range(NCH):
            t0 = c * CH
            LT = mLT[c % 2]
            P.dma("sp", LT.v(), mT[c])
            ot = f32t[6]
            P.dma("sp", ot.v(), mtok[c, :, 2, :])
            qb, kb = b16t[2], b16t[3]
            P.ts("dve", qb[:, 0:128], LT[:, 0, :], 0.125, ALU.mult)
            P.copy("pool", kb[:, 0:128], LT[:, 1, :])
            scs = (ps[3], ps[4])
            pos = (ps[5], ps[6])
            for h in range(2):
                hs = slice(h * 64, (h + 1) * 64)
                P.mm(scs[h][:, 0:128], kb[hs, 0:128], qb[hs, 0:128])
            for d in range(2):
                for h in range(2):
                    hs = slice(h * 64, (h + 1) * 64)
                    j = 2 * d + h
                    pt = b16t[4 + j]
                    P.stt("dve", pt[:, 0:128], scs[h][:, 0:128], EAll[:, c, j:j + 1], C[:, 5 + d, :], ALU.mult, ALU.mult)
                    dst = pos[h][:, d * VW:(d + 1) * VW]
                    P.mm(dst, pt[:, 0:128], vaug[:, c, h * VW:(h + 1) * VW], start=True, stop=False)
                    P.mm(dst, qb[hs, 0:128], mC[d][hs, c, :], start=False, stop=True)
            for d in range(2):
                for h in range(2):
                    hs = slice(h * 64, (h + 1) * 64)
                    j = 2 * d + h
                    P.ts("dve", na[:, j, 0:65], pos[h][:, d * VW:d * VW + 65], EAll[:, c, 4 + j:5 + j], ALU.mult)
                    den = sm[:, 8 + j:9 + j]
                    ng = sm[:, 12 + j:13 + j]
                    P.ts("dve", ng, na[:, j, 64:65], -1.0, ALU.mult)
                    P.tt("dve", den, na[:, j, 64:65], ng, ALU.max)
                    P.ts("dve", den, den, 1.0, ALU.max)
                    P.recip(den, den)
                    P.ts("pool", Hh[d][:, hs], na[:, j, 0:64], den, ALU.mult)
            hsum = f32t[7]
            P.tt("dve", hsum.v(), Hh[0].v(), Hh[1].v(), ALU.add)
            r = head_rms_scale(hsum, 64, EPS)
            P.act(ot.v(), ot.v(), AF.Sigmoid)
            y = f32t[8]
            for h in range(2):
                hs = slice(h * 64, (h + 1) * 64)
                P.stt("dve", y[:, hs], hsum[:, hs], r[:, h:h + 1], ot[:, hs], ALU.mult, ALU.mult)
            P.tt("pool", y.v(), y.v(), nw_t.v(), ALU.mult)
            outev.append(P.dma("pool", ymls[t0:t0 + CH, :], y.v()))

    P.emit(outev)
    return nc


def rope_np(S):
    inv = (np.float32(10000.0) ** (-(np.arange(0, 64, 2, dtype=np.float32)) / np.float32(64))).astype(np.float32)
    ang = (np.arange(S, dtype=np.float32)[:, None] * inv[None, :]).astype(np.float32)
    return np.cos(ang.astype(np.float64)).astype(np.float32), np.sin(ang.astype(np.float64)).astype(np.float32)


def mixer_consts():
    s = np.arange(128)[:, None]
    t = np.arange(128)[None, :]
    c = np.zeros((128, 8, 128), np.float32)
    c[:, 0] = np.eye(128)
    c[:, 1] = np.where(t >= s, t - s, 1e6)
    c[:, 2] = np.where(s > t, s - t, 1e6)
    c[:, 3] = np.broadcast_to(t + 1, (128, 128))
    c[:, 4] = np.broadcast_to(128 - t, (128, 128))
    c[:, 5] = (s <= t)
    c[:, 6] = (s >= t)
    c[:, 7] = 1.0
    cc = np.zeros((128, 4), np.float32)
    cc[:, 0] = 127 - np.arange(128)
    cc[:, 1] = np.arange(128)
    cc[:, 2] = 1.0
    return c, cc


GW = 256
OFF = {"cb": 0, "cc": 256, "ch": 512, "rq": 768, "rk": 1024, "rv": 1280, "rg": 1536, "zw": 1792,
       "mq": 2752, "mk": 3008, "mv": 3264, "mo": 3520, "mi": 3776, "mf": 3784}


def prep_mix_inputs(zb, prm, l, hp, which):
    S = zb.shape[0]
    A = np.ascontiguousarray
    hsl = slice(hp * 128, (hp + 1) * 128)
    col = lambda name: zb[:, OFF[name]:OFF[name] + GW][:, hsl]
    c, cc = mixer_consts()
    m = {"cst": c, "cstc": cc}

    def swap(x):
        x4 = x.reshape(S, 2, 2, 32)
        return x4[:, :, ::-1, :].reshape(S, 128)

    if "conv" in which:
        m["cA"] = A(np.stack([col("cb").T, col("cc").T, col("ch").T]))
        cw = np.zeros((128, 4), np.float32)
        cw[:, 0:3] = prm["conv_w"][l][:, hsl].T
        cw[:, 3] = prm["conv_b"][l][hsl]
        m["cw"] = cw
    if "ret" in which:
        cos, sin = rope_np(S)
        cos_tok = np.tile(cos, (1, 4))
        sin_tok = np.tile(np.concatenate([-sin, sin], axis=1), (1, 2))
        q, k = col("rq"), col("rk")
        NCH = S // CH
        fm = np.stack([q.T, swap(q).T, k.T, swap(k).T, cos_tok.T, sin_tok.T])
        m["rT"] = A(fm.reshape(6, 128, NCH, CH).transpose(2, 1, 0, 3))
        tm = np.stack([k, swap(k), col("rv"), col("rg"), cos_tok, sin_tok])
        m["rtok"] = A(tm.reshape(6, NCH, CH, 128).transpose(1, 2, 0, 3))
        dl = prm["ret_decay_logit"][l]
        rdl = np.zeros((128, 6), np.float32)
        for d in range(2):
            for h in range(2):
                rdl[:, 2 * d + h] = dl[d, hp * 2 + h]
            rdl[:, 4 + d] = np.repeat(dl[d, hp * 2:hp * 2 + 2], 64)
        m["rdl"] = rdl
    if "mlstm" in which:
        NCH = S // CH
        q, k = col("mq"), col("mk")
        fm = np.stack([q.T, k.T])
        m["mT"] = A(fm.reshape(2, 128, NCH, CH).transpose(2, 1, 0, 3))
        tm = np.stack([k, col("mv"), col("mo")])
        m["mtok"] = A(tm.reshape(3, NCH, CH, 128).transpose(1, 2, 0, 3))
        gcols = []
        for nm in ("mi", "mf"):
            for d in range(2):
                for h in range(2):
                    gcols.append(OFF[nm] + d * 4 + hp * 2 + h)
        m["mgate"] = A(zb[:, gcols].reshape(NCH, CH, 8))
        mb = np.zeros((128, 8), np.float32)
        for i, nm in enumerate(("mlstm_i_bias", "mlstm_f_bias")):
            for d in range(2):
                for h in range(2):
                    mb[:, i * 4 + d * 2 + h] = prm[nm][l][d, hp * 2 + h]
        m["mbias"] = mb
        m["mnw"] = A(np.broadcast_to(prm["mlstm_norm_w"][l][hsl][None, :], (128, 128)))
    if "rwkv" in which:
        NCH = S // CH
        base = OFF["zw"]
        rel = np.concatenate([np.arange(0, 256)[hsl], 256 + np.arange(0, 256)[hsl], 512 + np.arange(0, 256)[hsl],
                              np.arange(768, 960)])
        zc = zb[:, base + rel]
        zp = np.zeros_like(zc)
        zn = np.zeros_like(zc)
        zp[1:] = zc[:-1]
        zn[:-1] = zc[1:]
        m["wz"] = A(np.stack([zc, zp, zn]).reshape(3, NCH, CH, 576).transpose(1, 2, 0, 3))
        m["wmu"] = A(np.broadcast_to(prm["rwkv_mu"][l][rel][None, :], (128, 576)))
        rows = [prm["rwkv_w0"][l][0][hsl], prm["rwkv_w0"][l][1][hsl], prm["rwkv_a0"][l][0][hsl], prm["rwkv_a0"][l][1][hsl],
                prm["rwkv_k_k"][l][hsl], prm["rwkv_k_a"][l][hsl], prm["rwkv_r_k"][l].reshape(256)[hsl],
                prm["rwkv_ln_w"][l][hsl], prm["rwkv_ln_b"][l][hsl]]
        m["wpar"] = A(np.broadcast_to(np.stack(rows)[None, :, :], (128, 9, 128)))
        m["wlow"] = A(np.stack([prm["rwkv_w2"][l].reshape(64, 256)[:, hsl], prm["rwkv_a2"][l].reshape(64, 256)[:, hsl],
                                prm["rwkv_g2"][l][:, hsl]], axis=1))
    return m


N_CORES = 8
TOK_PER_CORE = 4096
SEQ = 8192


def _run(nc, in_maps):
    return run_bass_kernel_spmd(nc, in_maps, core_ids=list(range(N_CORES))).results


def kernel(**inp):
    inp = {k: np.asarray(v, dtype=np.float32) for k, v in inp.items()}
    depth = inp["norm_g"].shape[0]
    ident = np.eye(128, dtype=np.float32)
    which = ("conv", "ret", "rwkv", "mlstm")
    nc_mix = build_mixers(SEQ, which)
    x_sh = [np.ascontiguousarray(s_) for s_ in np.split(inp["x"].reshape(-1, D), N_CORES, axis=0)]

    def ffn_maps(l, i, tag):
        wg, wu, wd, gcol, grow = lay_ffn(inp["ffn_w_gate"][l, i], inp["ffn_w_up"][l, i], inp["ffn_w_down"][l, i],
                                         inp["norm_g"][l, 2 * i if i == 0 else 4], inp["norm_g"][l, 1 if i == 0 else 5])
        return {"wg" + tag: wg, "wu" + tag: wu, "wd" + tag: wd, "gcol" + tag: gcol, "grow" + tag: grow}

    def proj_maps(l):
        win, gcolp = lay_win(inp["w_in"][l], inp["norm_g"][l, 2])
        return {"win": win, "gcol_p": gcolp}

    def wout_maps(l):
        wo, grow = lay_wout(inp["w_out"][l], inp["norm_g"][l, 3])
        return {"wo": wo, "grow_o": grow}

    nc_a = build_token_pass(TOK_PER_CORE, ["ffnA", "proj"], ["A"])
    common = {"ident": ident, **ffn_maps(0, 0, "A"), **proj_maps(0)}
    res = _run(nc_a, [{"x": x_sh[c], **common} for c in range(N_CORES)])
    for l in range(depth):
        x_sh = [r["xo"] for r in res]
        z = np.concatenate([r["z"] for r in res], axis=0).reshape(4, SEQ, PT)
        prm = {k: v for k, v in inp.items() if k != "x"}
        rm = _run(nc_mix, [prep_mix_inputs(z[c // 2], prm, l, c % 2, which) for c in range(N_CORES)])
        ycat = np.zeros((4, SEQ, D), np.float32)
        for c in range(N_CORES):
            b, hp = c // 2, c % 2
            for mi_, (nm, tr) in enumerate((("y_conv", True), ("y_ret", False), ("y_rwkv", False), ("y_mlstm", False))):
                blk = rm[c][nm].T if tr else rm[c][nm]
                ycat[b, :, mi_ * GW + hp * 128:mi_ * GW + (hp + 1) * 128] = blk
        y_sh = np.split(ycat.reshape(-1, D), N_CORES, axis=0)
        if l + 1 < depth:
            nc_c = build_token_pass(TOK_PER_CORE, ["wout", "ffnB", "ffnA2", "proj"], ["B", "A2"])
            common = {"ident": ident, **wout_maps(l), **ffn_maps(l, 1, "B"), **ffn_maps(l + 1, 0, "A2"), **proj_maps(l + 1)}
        else:
            nc_c = build_token_pass(TOK_PER_CORE, ["wout", "ffnB"], ["B"])
            common = {"ident": ident, **wout_maps(l), **ffn_maps(l, 1, "B")}
        res = _run(nc_c, [{"x": x_sh[c], "ycat": np.ascontiguousarray(y_sh[c]), **common} for c in range(N_CORES)])
    out = np.concatenate([r["xo"] for r in res], axis=0).reshape(4, SEQ, D)
    return out.astype(np.float32)
```
